# Optimizing a Trainium2 kernel written in Bass

```python
import numpy as np
import jax, jax.numpy as jnp
from jax import lax

D_MODEL = 2048
BATCH = 8
SEQ = 2048
DEPTH = 2

CHUNK = 64
LEFT_CHUNKS = 8
BAND = (LEFT_CHUNKS + 1) * CHUNK
HEAD_DIM = 128
N_HEADS_TOTAL = D_MODEL // HEAD_DIM
N_HEADS_MEM = 4
N_HEADS_MAIN = N_HEADS_TOTAL - N_HEADS_MEM
W_MAIN = N_HEADS_MAIN * HEAD_DIM
W_MEM = N_HEADS_MEM * HEAD_DIM
N_MEM = 256
D_FF = 4 * D_MODEL
REL_CLIP = 256
Q_BLOCK = 128
N_A = DEPTH // 2
N_B = DEPTH - N_A
EPS = 1e-6

kernel_name = "yoco_chunked_relpos_stickbreaking_memory_block"


def rms_norm(x, g):
    xf = x.astype(jnp.float32)
    y = xf * lax.rsqrt(jnp.mean(xf * xf, axis=-1, keepdims=True) + EPS)
    return (y * g.astype(jnp.float32)).astype(x.dtype)


def split_heads(t, n_heads):
    return t.reshape(t.shape[:-1] + (n_heads, HEAD_DIM))


def rel_index():
    a = np.arange(CHUNK)[:, None, None]
    i = np.arange(LEFT_CHUNKS + 1)[None, :, None]
    b = np.arange(CHUNK)[None, None, :]
    dist = (LEFT_CHUNKS - i) * CHUNK + a - b
    return (np.clip(dist, -REL_CLIP, REL_CLIP) + REL_CLIP).reshape(CHUNK, BAND)


def chunk_band_attention(q, k, v, rel_table):
    B, S, H, Dh = q.shape
    nc = S // CHUNK
    pad = ((0, 0), (LEFT_CHUNKS, 0), (0, 0), (0, 0), (0, 0))
    qc = q.reshape(B, nc, CHUNK, H, Dh)
    kc = jnp.pad(k.reshape(B, nc, CHUNK, H, Dh), pad)
    vc = jnp.pad(v.reshape(B, nc, CHUNK, H, Dh), pad)
    scale = HEAD_DIM ** -0.5
    scores = jnp.concatenate(
        [jnp.einsum('bnqhd,bnkhd->bhnqk', qc, kc[:, i:i + nc]) for i in range(LEFT_CHUNKS + 1)],
        axis=-1).astype(jnp.float32) * scale
    bias = jnp.transpose(rel_table[jnp.asarray(rel_index())], (2, 0, 1)).astype(jnp.float32)
    chunk_id = np.arange(nc)[:, None]
    offset = np.repeat(np.arange(LEFT_CHUNKS + 1), CHUNK)[None, :]
    valid = jnp.asarray(chunk_id - LEFT_CHUNKS + offset >= 0)
    scores = jnp.where(valid[None, None, :, None, :], scores + bias[None, :, None], -jnp.inf)
    p = jax.nn.softmax(scores, axis=-1).astype(v.dtype)
    p = p.reshape(B, H, nc, CHUNK, LEFT_CHUNKS + 1, CHUNK)
    out = jnp.einsum('bhnqk,bnkhd->bnqhd', p[..., 0, :], vc[:, 0:nc])
    for i in range(1, LEFT_CHUNKS + 1):
        out = out + jnp.einsum('bhnqk,bnkhd->bnqhd', p[..., i, :], vc[:, i:i + nc])
    return out.reshape(B, S, H * Dh)


def stick_breaking_attention(q, k, v):
    B, S, H, Dh = q.shape
    scale = HEAD_DIM ** -0.5
    outs = []
    for s0 in range(0, S, Q_BLOCK):
        s1 = s0 + Q_BLOCK
        z = jnp.einsum('bqhd,bkhd->bhqk', q[:, s0:s1], k[:, :s1]).astype(jnp.float32) * scale
        t_idx = s0 + jnp.arange(Q_BLOCK)[:, None]
        s_idx = jnp.arange(s1)[None, :]
        causal = s_idx < t_idx
        sp = jnp.where(causal, jax.nn.softplus(z), 0.0)
        stick = lax.cumsum(sp, axis=3, reverse=True) - sp
        a = jnp.where(causal, jnp.exp(jax.nn.log_sigmoid(z) - stick), 0.0).astype(v.dtype)
        outs.append(jnp.einsum('bhqk,bkhd->bqhd', a, v[:, :s1]))
    return jnp.concatenate(outs, axis=1).reshape(B, S, H * Dh)


def memory_attention(qm, mem_n, w_kv, g_q, g_k):
    B, S = qm.shape[0], qm.shape[1]
    kv = mem_n @ w_kv
    km = rms_norm(split_heads(kv[..., :W_MEM], N_HEADS_MEM), g_k)
    vm = split_heads(kv[..., W_MEM:], N_HEADS_MEM)
    qm = rms_norm(qm, g_q)
    s = jnp.einsum('bqhd,bkhd->bhqk', qm, km).astype(jnp.float32) * (HEAD_DIM ** -0.5)
    p = jax.nn.softmax(s, axis=-1).astype(vm.dtype)
    return jnp.einsum('bhqk,bkhd->bqhd', p, vm).reshape(B, S, W_MEM)


def sq_relu_mlp(h, g, w_in, w_out):
    return jnp.square(jax.nn.relu(rms_norm(h, g) @ w_in)) @ w_out


def setup_inputs(seed: int = 0) -> dict:
    key = jax.random.key(seed)
    ks = jax.random.split(key, 16)
    f32 = jnp.float32
    nrm = lambda k, shape, s: jax.random.normal(k, shape, f32) * s
    gain = lambda k, shape: 1.0 + 0.05 * jax.random.normal(k, shape, f32)
    return {
        "x": nrm(ks[0], (BATCH, SEQ, D_MODEL), 1.0),
        "mem": nrm(ks[1], (BATCH, N_MEM, D_MODEL), 1.0),
        "norm_attn": gain(ks[2], (DEPTH, D_MODEL)),
        "norm_mem": gain(ks[3], (DEPTH, D_MODEL)),
        "norm_mlp": gain(ks[4], (DEPTH, D_MODEL)),
        "w_in_a": nrm(ks[5], (N_A, D_MODEL, 3 * W_MAIN + W_MEM), D_MODEL ** -0.5),
        "qk_gain_a": gain(ks[6], (N_A, 2, HEAD_DIM)),
        "rel_bias": nrm(ks[7], (N_A, 2 * REL_CLIP + 1, N_HEADS_MAIN), 0.1),
        "norm_kv": gain(ks[8], (D_MODEL,)),
        "w_kv_shared": nrm(ks[9], (D_MODEL, 2 * W_MAIN), D_MODEL ** -0.5),
        "w_q_b": nrm(ks[10], (N_B, D_MODEL, W_MAIN + W_MEM), D_MODEL ** -0.5),
        "w_mem_kv": nrm(ks[11], (DEPTH, D_MODEL, 2 * W_MEM), D_MODEL ** -0.5),
        "qk_gain_mem": gain(ks[12], (DEPTH, 2, HEAD_DIM)),
        "w_o": nrm(ks[13], (DEPTH, D_MODEL, D_MODEL), D_MODEL ** -0.5),
        "w_mlp_in": nrm(ks[14], (DEPTH, D_MODEL, D_FF), D_MODEL ** -0.5),
        "w_mlp_out": nrm(ks[15], (DEPTH, D_FF, D_MODEL), D_FF ** -0.5),
    }


def reference(x, mem, norm_attn, norm_mem, norm_mlp, w_in_a, qk_gain_a, rel_bias,
              norm_kv, w_kv_shared, w_q_b, w_mem_kv, qk_gain_mem, w_o, w_mlp_in, w_mlp_out):
    h = x
    k_sb = None
    v_sb = None
    for layer in range(DEPTH):
        xn = rms_norm(h, norm_attn[layer])
        mem_n = rms_norm(mem, norm_mem[layer])
        if layer < N_A:
            proj = xn @ w_in_a[layer]
            q = rms_norm(split_heads(proj[..., :W_MAIN], N_HEADS_MAIN), qk_gain_a[layer, 0])
            k = rms_norm(split_heads(proj[..., W_MAIN:2 * W_MAIN], N_HEADS_MAIN), qk_gain_a[layer, 1])
            v = split_heads(proj[..., 2 * W_MAIN:3 * W_MAIN], N_HEADS_MAIN)
            y_main = chunk_band_attention(q, k, v, rel_bias[layer])
            qm = proj[..., 3 * W_MAIN:]
        else:
            if layer == N_A:
                kv = rms_norm(h, norm_kv) @ w_kv_shared
                k_sb = split_heads(kv[..., :W_MAIN], N_HEADS_MAIN)
                v_sb = split_heads(kv[..., W_MAIN:], N_HEADS_MAIN)
            proj = xn @ w_q_b[layer - N_A]
            q = split_heads(proj[..., :W_MAIN], N_HEADS_MAIN)
            y_main = stick_breaking_attention(q, k_sb, v_sb)
            qm = proj[..., W_MAIN:]
        y_mem = memory_attention(split_heads(qm, N_HEADS_MEM), mem_n, w_mem_kv[layer],
                                 qk_gain_mem[layer, 0], qk_gain_mem[layer, 1])
        h = h + jnp.concatenate([y_main, y_mem], axis=-1) @ w_o[layer]
        h = h + sq_relu_mlp(h, norm_mlp[layer], w_mlp_in[layer], w_mlp_out[layer])
    return h
```

```python
import numpy as np
from contextlib import ExitStack
import concourse.bass as bass
import concourse.mybir as mybir
from concourse.bass_utils import run_bass_kernel_spmd

F32 = mybir.dt.float32
BF16 = mybir.dt.bfloat16
AF = mybir.ActivationFunctionType
ALU = mybir.AluOpType

D = 2048
SEQ = 2048
T = 512
NT = SEQ // T
NCH = D // 128
HD = 128
NH = 12
NHM = 4
NMEM = 256
DFF = 8192
EPS = 1e-6
SCALE = HD ** -0.5
NEG = -30000.0

G_ATTN = (0, 16)
G_MEM = (32, 48)
G_MLP = (64, 80)
G_KV = 96
G_QA = 112
G_QM = 114
NG = 118
C_ID, C_ONE, C_TRI, C_CMP, C_MASK = 0, 128, 256, 384, 512
NCST = 512 + 4 * 512


class Buf:
    def __init__(self, name):
        self.name = name
        self.w = None
        self.r = {}


class Sched:
    ENG = ("pe", "dve", "act", "pool", "sp")

    def __init__(self, nc, es, n_dma_sems):
        self.nc = nc
        self.e = {"pe": nc.tensor, "dve": nc.vector, "act": nc.scalar, "pool": nc.gpsimd, "sp": nc.sync}
        self.sem = {k: es.enter_context(nc.semaphore("s_" + k)) for k in self.ENG}
        self.cnt = {k: 0 for k in self.ENG}
        self.dsem = [es.enter_context(nc.semaphore("d%d" % i)) for i in range(n_dma_sems)]
        self.dcnt = [0] * n_dma_sems
        self.ndsem = 0
        self.waited = {k: {} for k in self.ENG}

    def alloc_dsem(self):
        self.ndsem += 1
        return self.ndsem - 1

    def _wait(self, eng, tok):
        if tok is None:
            return
        if tok[0] == "c":
            if eng == "pe" and tok[1] == "pe":
                return
            key, sem, val = ("c", tok[1]), self.sem[tok[1]], tok[2]
        else:
            key, sem, val = ("d", tok[1]), self.dsem[tok[1]], tok[2]
        if self.waited[eng].get(key, 0) >= val:
            return
        self.waited[eng][key] = val
        self.e[eng].wait_ge(sem, val)

    def _deps(self, eng, reads, writes):
        for b in reads:
            self._wait(eng, b.w)
        for b in writes:
            self._wait(eng, b.w)
            for t in list(b.r.values()):
                self._wait(eng, t)

    def _commit(self, tok, reads, writes):
        for b in reads:
            b.r[(tok[0], tok[1])] = tok
        for b in writes:
            b.w = tok
            b.r = {}

    def op(self, eng, fn, reads=(), writes=()):
        self._deps(eng, reads, writes)
        ins = fn()
        self.cnt[eng] += 1
        ins.then_inc(self.sem[eng], 1)
        self._commit(("c", eng, self.cnt[eng]), reads, writes)

    def dma(self, q, out, in_, reads=(), writes=(), dsem=None):
        self._deps(q, reads, writes)
        self.dcnt[dsem] += 16
        self.e[q].dma_start(out=out, in_=in_).then_inc(self.dsem[dsem], 16)
        self._commit(("d", dsem, self.dcnt[dsem]), reads, writes)

    def final_wait(self, eng, bufs):
        for b in bufs:
            self._wait(eng, b.w)


def pipeline(items, stages, skew=1, reverse=False):
    n, ns = len(items), len(stages)
    for step in range(n + (ns - 1) * skew):
        for s in (range(ns - 1, -1, -1) if reverse else range(ns)):
            k = step - s * skew
            if 0 <= k < n:
                stages[s](items[k])


def build_program(n_tiles=NT, n_layers=2, debug=False):
    nc = bass.Bass("TRN2", target_bir_lowering=False)
    dram_in = lambda n, sh: nc.dram_tensor(n, sh, F32, kind="ExternalInput").ap()
    x_d = dram_in("x", [SEQ, D])
    mem_d = dram_in("mem", [NMEM, D])
    w_in_a = dram_in("w_in_a", [D, 5120])
    w_kv = dram_in("w_kv_shared", [D, 3072])
    w_q_b = dram_in("w_q_b", [D, 2048])
    w_mem = dram_in("w_mem_kv", [2, D, 1024])
    w_o = dram_in("w_o", [2, D, D])
    w_mi = dram_in("w_mlp_in", [2, D, DFF])
    w_mo = dram_in("w_mlp_out", [2, DFF, D])
    gains_d = dram_in("gains", [128, NG])
    bias_d = dram_in("biasT", [128, NH * 5 * 128])
    cst_d = dram_in("cst", [128, NCST])
    out_d = nc.dram_tensor("out", [SEQ, D], F32, kind="ExternalOutput").ap()
    Kc = [nc.dram_tensor("kc%d" % l, [NH, 128, SEQ], BF16).ap() for l in range(2)]
    Vc = [nc.dram_tensor("vc%d" % l, [NH, 128, SEQ // 128, 128], BF16).ap() for l in range(2)]

    with ExitStack() as es:
        S = Sched(nc, es, n_dma_sems=64)
        sb = lambda n, sh, dt: es.enter_context(nc.sbuf_tensor(n, sh, dt))
        ps = lambda n, sh, dt: es.enter_context(nc.psum_tensor(n, sh, dt))

        hT = sb("hT", [128, NCH, T], F32)
        BhT = [Buf("hT%d" % c) for c in range(NCH)]
        xnT = sb("xnT", [128, NCH, T], BF16)
        BxnT = [Buf("xn%d" % c) for c in range(NCH)]
        R1 = sb("R1", [128, 32, T], BF16)
        BR1 = [Buf("R1_%d" % i) for i in range(32)]
        R1f = R1[:].bitcast(F32)
        kst = [sb("kst%d" % i, [128, SEQ], BF16) for i in range(2)]
        vst = [sb("vst%d" % i, [128, SEQ // 128, 128], BF16) for i in range(2)]
        Bkst = [Buf("kst0"), Buf("kst1")]
        Bvst = [Buf("vst0"), Buf("vst1")]
        kvst_d = [S.alloc_dsem() for _ in range(4)]
        ktile = [sb("ktile%d" % i, [128, T], BF16) for i in range(2)]
        Bktile = [Buf("kt0"), Buf("kt1")]
        NVT = 4
        vtile = [sb("vtile%d" % i, [128, 256], BF16) for i in range(NVT)]
        Bvtile = [Buf("vt%d" % i) for i in range(NVT)]
        biasT = sb("biasT_sb", [128, NH * 5 * 128], BF16)
        Bbias = Buf("bias")
        kmT = [sb("kmT%d" % l, [128, NHM, NMEM], BF16) for l in range(2)]
        vm = [sb("vm%d" % l, [128, 2, 512], BF16) for l in range(2)]
        Bkm = [Buf("km0"), Buf("km1")]
        Bvm = [Buf("vm0"), Buf("vm1")]
        NF = 8
        f32p = [sb("f32p%d" % i, [128, T], F32) for i in range(NF)]
        Bf32p = [Buf("f32p%d" % i) for i in range(NF)]
        NB = 10
        b16p = [sb("b16p%d" % i, [128, T], BF16) for i in range(NB)]
        Bb16p = [Buf("b16p%d" % i) for i in range(NB)]
        rstd = sb("rstd", [128, T], F32)
        Brstd = Buf("rstd")
        gains = sb("gains_sb", [128, NG], F32)
        Bg = Buf("gains")
        cst = sb("cst_sb", [128, C_MASK], F32)
        Bc = Buf("cst")
        cstb = sb("cstb", [128, NCST], BF16)
        Bcb = Buf("cstb")
        rem = int(nc.sbuf_bytes_remaining)
        NSLAB = max(3, min(5, (rem - 4096) // 8192))
        slab = [sb("slab%d" % i, [128, 16, 256], BF16) for i in range(NSLAB)]
        Bslab = [Buf("slab%d" % i) for i in range(NSLAB)]
        slab_d = [S.alloc_dsem() for _ in range(NSLAB)]
        pb = [ps("pb%d" % i, [128, T], F32) for i in range(8)]
        Bpb = [Buf("pb%d" % i) for i in range(8)]

        ident = cst[:, C_ID:C_ID + 128]
        ones_f = cst[:, C_ONE:C_ONE + 128]
        ones_b = cstb[:, C_ONE:C_ONE + 128]
        tri_b = cstb[:, C_TRI:C_TRI + 128]
        cmp_b = cstb[:, C_CMP:C_CMP + 128]
        maskb = lambda r: cstb[:, C_MASK + r * 512:C_MASK + (r + 1) * 512]

        st = {"f": 0, "b": 0, "slab": 0}
        dbg_list = []

        def dbg(name, ap, bufs, dt):
            if not debug:
                return
            shp = list(ap.shape)
            dtn = nc.dram_tensor("dbg_" + name, shp, dt, kind="ExternalOutput").ap()
            ds = S.alloc_dsem()
            B = Buf("dbg_" + name)
            S.dma("sp", dtn, ap, reads=bufs, writes=[B], dsem=ds)
            dbg_list.append((B, ds))

        def tf():
            i = st["f"] % NF
            st["f"] += 1
            return f32p[i], Bf32p[i]

        def tb():
            i = st["b"] % NB
            st["b"] += 1
            return b16p[i], Bb16p[i]

        S.dma("sp", gains[:], gains_d[:, :], writes=[Bg], dsem=S.alloc_dsem())
        S.dma("sp", cst[:], cst_d[:, 0:C_MASK], writes=[Bc], dsem=S.alloc_dsem())
        S.dma("pool", cstb[:], cst_d[:, :], writes=[Bcb], dsem=S.alloc_dsem())
        S.dma("pool", biasT[:], bias_d[:, :], writes=[Bbias], dsem=S.alloc_dsem())
        S.op("dve", lambda: nc.vector.tensor_scalar(gains[:, 0:G_QA], gains[:, 0:G_QA], float(np.sqrt(D)), None, ALU.mult),
             reads=[Bg], writes=[Bg])
        S.op("dve", lambda: nc.vector.tensor_scalar(gains[:, G_QA:NG], gains[:, G_QA:NG], float(np.sqrt(HD)), None, ALU.mult),
             reads=[Bg], writes=[Bg])

        def load_slab(src_ap):
            i = st["slab"] % NSLAB
            st["slab"] += 1
            S.dma("pool", slab[i][:], src_ap, writes=[Bslab[i]], dsem=slab_d[i])
            return slab[i], Bslab[i]

        def wview(w2d, r0, c0):
            return w2d[r0:r0 + 2048, c0:c0 + 256].rearrange("(kc p) f -> p kc f", p=128)

        gemm_banks = [0, 1, 2, 3, 4, 5]
        gstate = {"bank": 0, "stat": 0}

        def next_bank():
            i = gemm_banks[gstate["bank"] % len(gemm_banks)]
            gstate["bank"] += 1
            return i

        def next_stat():
            i = 6 + gstate["stat"] % 2
            gstate["stat"] += 1
            return i

        def gemm_f(w2d, r0, c0, nfb, kslabs, act, epi, N=T):
            pending = []
            for j in range(nfb // 2):
                banks = [next_bank(), next_bank()]
                for ks in range(kslabs):
                    sl, Bsl = load_slab(wview(w2d, r0 + ks * 2048, c0 + j * 256))
                    for f in range(2):
                        def mm(f=f, ks=ks, sl=sl):
                            ins = None
                            for kc in range(16):
                                a, _ = act(ks * 16 + kc)
                                ins = nc.tensor.matmul(pb[banks[f]][:, 0:N], lhsT=sl[:, kc, f * 128:(f + 1) * 128], rhs=a,
                                                       start=(ks == 0 and kc == 0), stop=(ks == kslabs - 1 and kc == 15))
                            return ins
                        if j == 0 and ks == 0 and f == 0:
                            for kc in range(16):
                                a, Ba = act(kc)
                                S.op("pe", lambda: nc.tensor.matmul(pb[banks[f]][:, 0:N], lhsT=sl[:, kc, f * 128:(f + 1) * 128], rhs=a,
                                                                    start=(kc == 0), stop=(kslabs == 1 and kc == 15)),
                                     reads=[Bsl, Ba], writes=[Bpb[banks[f]]])
                            continue
                        rb = [Bsl] + [act(ks * 16 + kc)[1] for kc in range(16)]
                        S.op("pe", mm, reads=rb, writes=[Bpb[banks[f]]])
                newp = []
                for f in range(2):
                    d = epi(2 * j + f, banks[f])
                    if d is not None:
                        newp.append(d)
                for d in pending:
                    d()
                pending = newp
            for d in pending:
                d()

        def gemm_t(w2d, c0, nslabs, act, epi):
            for j in range(nslabs):
                sl, Bsl = load_slab(wview(w2d, 0, c0 + j * 256))
                for tblk in range(T // 128):
                    bk = next_bank()

                    def mm(sl=sl, tblk=tblk, bk=bk):
                        ins = None
                        for kc in range(16):
                            ins = nc.tensor.matmul(pb[bk][:, 0:256], lhsT=act(kc)[0][:, tblk * 128:(tblk + 1) * 128], rhs=sl[:, kc, :],
                                                   start=(kc == 0), stop=(kc == 15))
                        return ins
                    S.op("pe", mm, reads=[Bsl] + [act(kc)[1] for kc in range(16)], writes=[Bpb[bk]])
                    epi(j, tblk, bk)

        def act_xn(kc):
            return xnT[:, kc, :], BxnT[kc]

        def norm_main(gcol, N=T, src=None, Bsrc=None, dst=None, Bdst=None, stats="compute"):
            src = hT if src is None else src
            Bsrc = BhT if Bsrc is None else Bsrc
            dst = xnT if dst is None else dst
            Bdst = BxnT if Bdst is None else Bdst
            if stats == "compute":
                sbk = next_stat()
                for c in range(NCH):
                    sq, Bsq = tf()
                    S.op("act", lambda: nc.scalar.activation(sq[:, 0:N], src[:, c, 0:N], AF.Square), reads=[Bsrc[c]], writes=[Bsq])
                    S.op("pe", lambda: nc.tensor.matmul(pb[sbk][:, 0:N], lhsT=ones_f, rhs=sq[:, 0:N], start=(c == 0), stop=(c == NCH - 1)),
                         reads=[Bsq, Bc], writes=[Bpb[sbk]])
            elif stats == "bank7":
                sbk = 7
            if stats != "reuse":
                lt, Blt = tf()
                S.op("act", lambda: nc.scalar.activation(lt[:, 0:N], pb[sbk][:, 0:N], AF.Ln, bias=float(D * EPS)), reads=[Bpb[sbk]], writes=[Blt])
                S.op("act", lambda: nc.scalar.activation(rstd[:, 0:N], lt[:, 0:N], AF.Exp, scale=-0.5), reads=[Blt], writes=[Brstd])
            for c in range(NCH):
                S.op("dve", lambda: nc.vector.scalar_tensor_tensor(dst[:, c, 0:N], src[:, c, 0:N], gains[:, gcol + c:gcol + c + 1], rstd[:, 0:N],
                                                                   ALU.mult, ALU.mult),
                     reads=[Bsrc[c], Brstd, Bg], writes=[Bdst[c]])

        def headnorm_epi(bank, gcol, dst_ap, Bdst, N=T, after=None):
            sq, Bsq = tf()
            S.op("act", lambda: nc.scalar.activation(sq[:, 0:N], pb[bank][:, 0:N], AF.Square), reads=[Bpb[bank]], writes=[Bsq])

            def deferred():
                sbk = next_stat()
                S.op("pe", lambda: nc.tensor.matmul(pb[sbk][:, 0:N], lhsT=ones_f, rhs=sq[:, 0:N], start=True, stop=True),
                     reads=[Bsq, Bc], writes=[Bpb[sbk]])
                r, Br = sq, Bsq
                S.op("act", lambda: nc.scalar.activation(r[:, 0:N], pb[sbk][:, 0:N], AF.Ln, bias=float(HD * EPS)), reads=[Bpb[sbk]], writes=[Br])
                S.op("act", lambda: nc.scalar.activation(r[:, 0:N], r[:, 0:N], AF.Exp, scale=-0.5), reads=[Br], writes=[Br])
                S.op("dve", lambda: nc.vector.scalar_tensor_tensor(dst_ap, pb[bank][:, 0:N], gains[:, gcol:gcol + 1], r[:, 0:N], ALU.mult, ALU.mult),
                     reads=[Bpb[bank], Br, Bg], writes=[Bdst])
                if after is not None:
                    after()
            return deferred

        def softmax_attn(items):
            flat = []
            for it in items:
                nb = len(it["blocks"])
                for bi, blk in enumerate(it["blocks"]):
                    flat.append((it, bi, blk, bi == 0, bi == nb - 1))
            sc_banks = [0, 1, 2]
            cnt = {"s": 0}
            hold = {}

            def stA(e):
                it, bi, (kl, v, c0, c1, bias), first, last = e
                bk = sc_banks[cnt["s"] % 3]
                cnt["s"] += 1
                if first and it.get("load") is not None:
                    it["load"]()
                S.op("pe", lambda: nc.tensor.matmul(pb[bk][:, c0:c1], lhsT=kl, rhs=it["q"](c0, c1), start=True, stop=True),
                     reads=it["reads"], writes=[Bpb[bk]])
                p, Bp = tb()
                if bias is not None:
                    t, Bt = tf()
                    S.op("dve", lambda: nc.vector.scalar_tensor_tensor(t[:, c0:c1], pb[bk][:, c0:c1], SCALE, bias, ALU.mult, ALU.add),
                         reads=[Bpb[bk], Bbias], writes=[Bt])
                    S.op("act", lambda: nc.scalar.activation(p[:, c0:c1], t[:, c0:c1], AF.Exp), reads=[Bt], writes=[Bp])
                else:
                    S.op("act", lambda: nc.scalar.activation(p[:, c0:c1], pb[bk][:, c0:c1], AF.Exp, scale=SCALE), reads=[Bpb[bk]], writes=[Bp])
                hold[id(e)] = (p, Bp)

            def stB(e):
                it, bi, (kl, v, c0, c1, bias), first, last = e
                p, Bp = hold.pop(id(e))
                ob, db = it["ob"], it["db"]

                def mm():
                    nc.tensor.matmul(pb[db][:, c0:c1], lhsT=ones_b, rhs=p[:, c0:c1], start=first, stop=last, skip_group_check=True)
                    return nc.tensor.matmul(pb[ob][:, c0:c1], lhsT=v, rhs=p[:, c0:c1], start=first, stop=last, skip_group_check=True)
                S.op("pe", mm, reads=[Bp, Bcb] + it["reads"], writes=[Bpb[ob], Bpb[db]])
                if last:
                    r, Br = tf()
                    S.op("act", lambda: nc.scalar.activation(r[:], pb[db][:], AF.Ln), reads=[Bpb[db]], writes=[Br])
                    S.op("act", lambda: nc.scalar.activation(r[:], r[:], AF.Exp, scale=-1.0), reads=[Br], writes=[Br])
                    S.op("dve", lambda: nc.vector.tensor_tensor(it["dst"], pb[ob][:], r[:], ALU.mult), reads=[Bpb[ob], Br], writes=[it["Bdst"]])
            pipeline(flat, [stA, stB], skew=2)

        def load_kv(l, h, tok0, tok1, par):
            S.dma("sp", kst[par][:, tok0:tok1], Kc[l][h, :, tok0:tok1], reads=[b for t_ in range(tok0 // T, (tok1 - 1) // T + 1) for b in BKc[l][t_]],
                  writes=[Bkst[par]], dsem=kvst_d[par])
            S.dma("sp", vst[par][:, tok0 // 128:tok1 // 128, :], Vc[l][h, :, tok0 // 128:tok1 // 128, :],
                  reads=[b for t_ in range(tok0 // T, (tok1 - 1) // T + 1) for b in BVc[l][t_]], writes=[Bvst[par]], dsem=kvst_d[2 + par])

        BKc = [[[Buf("Kc%d_%d_%d" % (l, t_, p_)) for p_ in range(2)] for t_ in range(NT)] for l in range(2)]
        BVc = [[[Buf("Vc%d_%d_%d" % (l, t_, p_)) for p_ in range(NVT)] for t_ in range(NT)] for l in range(2)]
        kc_d = [S.alloc_dsem() for _ in range(2)]
        vc_d = [S.alloc_dsem() for _ in range(NVT)]
        kvst2_d = [S.alloc_dsem() for _ in range(4)]

        QT = lambda h: R1[:, h, :]
        QMT = lambda h: R1[:, 12 + h, :]
        YT = lambda j: R1[:, 16 + j, :]

        memT = lambda c: R1f[:, c, :]
        xs_d = [S.alloc_dsem() for _ in range(4)]
        xstage = [R1f[:, 16:24, :], R1f[:, 24:32, :], R1f[:, 0:8, :], R1f[:, 8:16, :]]
        Bxst = [BR1[16:24], BR1[24:32], BR1[0:8], BR1[8:16]]
        for blk in range(2):
            S.dma("sp", xstage[blk], mem_d[blk * 128:(blk + 1) * 128, :].rearrange("p (a b) -> p a b", b=256), writes=Bxst[blk], dsem=xs_d[blk])
            xs2 = xstage[blk].rearrange("p a b -> p (a b)")
            for g in range(4):
                bk = next_bank()

                def tr(g=g, bk=bk):
                    ins = None
                    for j in range(4):
                        c = g * 4 + j
                        ins = nc.tensor.transpose(pb[bk][:, j * 128:(j + 1) * 128], xs2[:, c * 128:(c + 1) * 128], ident)
                    return ins
                S.op("pe", tr, reads=Bxst[blk] + [Bc], writes=[Bpb[bk]])
                S.op("act", lambda: nc.scalar.copy(R1f[:, g * 4:g * 4 + 4, blk * 128:(blk + 1) * 128],
                                                   pb[bk][:].rearrange("p (j t) -> p j t", t=128)),
                     reads=[Bpb[bk]], writes=BR1[g * 4:g * 4 + 4])
        mem_n = xnT
        Bmem_n = BxnT
        for l in range(n_layers):
            norm_main(G_MEM[l], N=NMEM, src=R1f, Bsrc=BR1[0:16], dst=mem_n, Bdst=Bmem_n, stats=("compute" if l == 0 else "reuse"))
            act_m = lambda kc: (mem_n[:, kc, 0:NMEM], Bmem_n[kc])

            def epi_km(fb, bank, l=l):
                return headnorm_epi(bank, G_QM + 2 * l + 1, kmT[l][:, fb, :], Bkm[l], N=NMEM)
            gemm_f(w_mem[l], 0, 0, NHM, 1, act_m, epi_km, N=NMEM)
            for j in range(2):
                sl, Bsl = load_slab(wview(w_mem[l], 0, 512 + j * 256))
                for blk in range(2):
                    bk = next_bank()

                    def mm(sl=sl, blk=blk, bk=bk):
                        ins = None
                        for kc in range(16):
                            ins = nc.tensor.matmul(pb[bk][:, 0:256], lhsT=mem_n[:, kc, blk * 128:(blk + 1) * 128], rhs=sl[:, kc, :],
                                                   start=(kc == 0), stop=(kc == 15))
                        return ins
                    S.op("pe", mm, reads=[Bsl] + Bmem_n, writes=[Bpb[bk]])
                    S.op("act", lambda: nc.scalar.copy(vm[l][:, blk, j * 256:(j + 1) * 256], pb[bk][:, 0:256]), reads=[Bpb[bk]], writes=[Bvm[l]])

        os_d = [S.alloc_dsem() for _ in range(4)]
        Bout = Buf("out")
        for ti in range(n_tiles):
            t0 = ti * T
            for blk in range(4):
                s = blk
                S.dma("sp", xstage[s], x_d[t0 + blk * 128:t0 + (blk + 1) * 128, :].rearrange("p (a b) -> p a b", b=256), writes=Bxst[s], dsem=xs_d[s])
                xs2 = xstage[s].rearrange("p a b -> p (a b)")
                for g in range(4):
                    bk = next_bank()

                    def tr(g=g, bk=bk, xs2=xs2):
                        ins = None
                        for j in range(4):
                            c = g * 4 + j
                            ins = nc.tensor.transpose(pb[bk][:, j * 128:(j + 1) * 128], xs2[:, c * 128:(c + 1) * 128], ident)
                        return ins
                    S.op("pe", tr, reads=Bxst[s] + [Bc], writes=[Bpb[bk]])
                    S.op("act", lambda: nc.scalar.copy(hT[:, g * 4:g * 4 + 4, blk * 128:(blk + 1) * 128],
                                                       pb[bk][:].rearrange("p (j t) -> p j t", t=128)),
                         reads=[Bpb[bk]], writes=BhT[g * 4:g * 4 + 4])

            if ti == 0:
                dbg("h0", hT[:], BhT, F32)
            for l in range(n_layers):
                kv_w, kv_c0 = (w_in_a, 1536) if l == 0 else (w_kv, 0)
                if l == 1:
                    norm_main(G_KV, stats="bank7")
                else:
                    norm_main(G_ATTN[0])

                def k_epi(fb, bank, l=l):
                    kt, Bkt = ktile[fb % 2], Bktile[fb % 2]

                    def store(fb=fb, kt=kt, Bkt=Bkt):
                        S.dma("sp", Kc[l][fb, :, t0:t0 + T], kt[:], reads=[Bkt], writes=[BKc[l][ti][fb % 2]], dsem=kc_d[fb % 2])
                    if l == 0:
                        return headnorm_epi(bank, G_QA + 1, kt[:], Bkt, after=store)
                    S.op("act", lambda: nc.scalar.copy(kt[:], pb[bank][:]), reads=[Bpb[bank]], writes=[Bkt])
                    store()
                    return None

                def v_epi(j, tblk, bk, l=l):
                    vp = (j * 4 + tblk) % NVT
                    vt, Bvt = vtile[vp], Bvtile[vp]
                    S.op("act", lambda: nc.scalar.copy(vt[:], pb[bk][:, 0:256]), reads=[Bpb[bk]], writes=[Bvt])
                    S.dma("sp", Vc[l][2 * j:2 * j + 2, :, ti * 4 + tblk, :].rearrange("h p d -> p h d"), vt[:].rearrange("p (h d) -> p h d", d=128),
                          reads=[Bvt], writes=[BVc[l][ti][vp]], dsem=vc_d[vp])

                if l == 0:
                    def q_epi(fb, bank):
                        if fb < NH:
                            return headnorm_epi(bank, G_QA, QT(fb), BR1[fb])
                        return k_epi(fb - NH, bank)
                    gemm_f(w_in_a, 0, 0, 2 * NH, 1, act_xn, q_epi)
                    gemm_t(w_in_a, 3072, 6, act_xn, v_epi)

                    def qm_epi(fb, bank):
                        return headnorm_epi(bank, G_QM + 0, QMT(fb), BR1[12 + fb])
                    gemm_f(w_in_a, 0, 4608, NHM, 1, act_xn, qm_epi)
                else:
                    gemm_f(w_kv, 0, 0, NH, 1, act_xn, k_epi)
                    gemm_t(w_kv, 1536, 6, act_xn, v_epi)
                    norm_main(G_ATTN[1], stats="reuse")

                    def q1_epi(fb, bank):
                        if fb < NH:
                            S.op("act", lambda: nc.scalar.copy(QT(fb), pb[bank][:]), reads=[Bpb[bank]], writes=[BR1[fb]])
                            return None
                        return headnorm_epi(bank, G_QM + 2, QMT(fb - NH), BR1[12 + fb - NH])
                    gemm_f(w_q_b, 0, 0, NH + NHM, 1, act_xn, q1_epi)

                if ti == 0:
                    dbg("xn%d" % l, xnT[:], BxnT, BF16)
                    dbg("q%d" % l, R1[:, 0:16, :], BR1[0:16], BF16)
                if l == 0:
                    items = []
                    for h in range(NH):
                        par = h % 2
                        lo = max(0, t0 - T)
                        blocks = []
                        for kbi in range(8):
                            kb = 4 * (ti - 1) + kbi
                            if kb < 0:
                                continue
                            g_lo, g_hi = max(0, kbi - 4), min(3, kbi)
                            c0, c1 = g_lo * 128, (g_hi + 1) * 128
                            rlo = 4 - kbi + g_lo
                            bofs = (h * 5 + rlo) * 128
                            blocks.append((kst[par][:, kb * 128:(kb + 1) * 128], vst[par][:, kb, :], c0, c1, biasT[:, bofs:bofs + (c1 - c0)]))
                        items.append(dict(blocks=blocks, load=(lambda h=h, lo=lo, par=par: load_kv(0, h, lo, t0 + T, par)),
                                          q=(lambda c0, c1, h=h: R1[:, h, c0:c1]), reads=[BR1[h], Bkst[par], Bvst[par]],
                                          dst=YT(h), Bdst=BR1[16 + h], ob=3 + 2 * par, db=4 + 2 * par))
                    softmax_attn(items)
                else:
                    def stage_set(si):
                        if si < 2:
                            return kst[si], vst[si], [Bkst[si]], [Bvst[si]], kvst_d[si], kvst_d[2 + si]
                        b0 = (si - 2) * 8
                        kv_ = xnT[:, b0:b0 + 4, :].rearrange("p a b -> p (a b)")
                        vv_ = xnT[:, b0 + 4:b0 + 8, :].rearrange("p a (b c) -> p (a b) c", c=128)
                        return kv_, vv_, BxnT[b0:b0 + 4], BxnT[b0 + 4:b0 + 8], kvst2_d[(si - 2) * 2], kvst2_d[(si - 2) * 2 + 1]
                    flat = []
                    nkb = 4 * (ti + 1)
                    for hp in range(NH // 2):
                        for kb in range(nkb - 1, -1, -1):
                            for h in (2 * hp, 2 * hp + 1):
                                r = kb - 4 * ti
                                flat.append(dict(h=h, kb=kb, first=(kb == nkb - 1), last=(kb == 0), diag=(r >= 0), r=r,
                                                 c0=(r * 128 if r >= 0 else 0), ss=stage_set(h % 4)))
                    cz = {"n": 0, "e": 0, "x": 0, "sp": 0, "a": 0}

                    def s0(e):
                        h, kb, c0 = e["h"], e["kb"], e["c0"]
                        kv_, vv_, Bk_, Bv_, dk, dv = e["ss"]
                        if e["first"]:
                            ntok = t0 + T
                            S.dma("sp", kv_[:, 0:ntok], Kc[1][h, :, 0:ntok], reads=[b for t_ in range(ti + 1) for b in BKc[1][t_]], writes=Bk_, dsem=dk)
                            S.dma("sp", vv_[:, 0:ntok // 128, :], Vc[1][h, :, 0:ntok // 128, :], reads=[b for t_ in range(ti + 1) for b in BVc[1][t_]],
                                  writes=Bv_, dsem=dv)
                        bk = cz["n"] % 3
                        cz["n"] += 1
                        S.op("pe", lambda: nc.tensor.matmul(pb[bk][:, c0:T], lhsT=kv_[:, kb * 128:(kb + 1) * 128], rhs=R1[:, h, c0:T], start=True, stop=True),
                             reads=Bk_ + [BR1[h]], writes=[Bpb[bk]])
                        ei = cz["e"] % 6
                        cz["e"] += 1
                        ex, Bex = f32p[ei], Bf32p[ei]
                        S.op("act", lambda: nc.scalar.activation(ex[:, c0:T], pb[bk][:, c0:T], AF.Exp, scale=SCALE), reads=[Bpb[bk]], writes=[Bex])
                        if e["diag"]:
                            S.op("dve", lambda: nc.vector.tensor_tensor(ex[:, c0:T], ex[:, c0:T], maskb(e["r"])[:, c0:T], ALU.mult), reads=[Bex, Bcb], writes=[Bex])
                        e["ex"] = (ex, Bex)

                    def s1(e):
                        c0 = e["c0"]
                        ex, Bex = e["ex"]
                        si = cz["sp"] % 6
                        cz["sp"] += 1
                        spb, Bspb = b16p[si], Bb16p[si]
                        S.op("act", lambda: nc.scalar.activation(spb[:, c0:T], ex[:, c0:T], AF.Ln, bias=1.0), reads=[Bex], writes=[Bspb])
                        e["sp"] = (spb, Bspb)

                    def s2(e):
                        c0 = e["c0"]
                        ex, Bex = e["ex"]
                        spb, Bspb = e["sp"]
                        ab = 3 + e["h"] % 2
                        S.op("pe", lambda: nc.tensor.matmul(pb[ab][:, c0:T], lhsT=tri_b, rhs=spb[:, c0:T], start=e["first"], stop=False, skip_group_check=True),
                             reads=[Bspb, Bcb], writes=[Bpb[ab]])
                        xi = 6 + cz["x"] % 2
                        cz["x"] += 1
                        x2, Bx2 = f32p[xi], Bf32p[xi]
                        S.op("act", lambda: nc.scalar.activation(x2[:, c0:T], pb[ab][:, c0:T], AF.Exp, scale=-1.0), reads=[Bpb[ab]], writes=[Bx2])
                        ai = 6 + cz["a"] % 4
                        cz["a"] += 1
                        aT, BaT = b16p[ai], Bb16p[ai]
                        S.op("dve", lambda: nc.vector.tensor_tensor(aT[:, c0:T], ex[:, c0:T], x2[:, c0:T], ALU.mult), reads=[Bex, Bx2], writes=[BaT])
                        e["a"] = (aT, BaT)

                    def s3(e):
                        h, kb, c0 = e["h"], e["kb"], e["c0"]
                        kv_, vv_, Bk_, Bv_, dk, dv = e["ss"]
                        spb, Bspb = e["sp"]
                        aT, BaT = e["a"]
                        ab, ob = 3 + h % 2, 5 + h % 2

                        def mm():
                            if not e["last"]:
                                nc.tensor.matmul(pb[ab][:, c0:T], lhsT=cmp_b, rhs=spb[:, c0:T], start=False, stop=False, skip_group_check=True)
                            return nc.tensor.matmul(pb[ob][:, c0:T], lhsT=vv_[:, kb, :], rhs=aT[:, c0:T], start=e["first"], stop=e["last"], skip_group_check=True)
                        S.op("pe", mm, reads=[Bspb, BaT, Bcb] + Bv_, writes=[Bpb[ab], Bpb[ob]])
                        if e["last"]:
                            S.op("act", lambda: nc.scalar.copy(YT(h), pb[ob][:]), reads=[Bpb[ob]], writes=[BR1[16 + h]])
                    pipeline(flat, [s0, s1, s2, s3], skew=2, reverse=True)

                items = []
                for hm in range(NHM):
                    blocks = [(kmT[l][:, hm, b * 128:(b + 1) * 128], vm[l][:, b, hm * 128:(hm + 1) * 128], 0, T, None) for b in range(2)]
                    items.append(dict(blocks=blocks, q=(lambda c0, c1, hm=hm: R1[:, 12 + hm, c0:c1]), reads=[BR1[12 + hm], Bkm[l], Bvm[l]],
                                      dst=YT(NH + hm), Bdst=BR1[16 + NH + hm], ob=3 + 2 * (hm % 2), db=4 + 2 * (hm % 2)))
                softmax_attn(items)

                if ti == 0:
                    dbg("y%d" % l, R1[:, 16:32, :], BR1[16:32], BF16)
                def make_resid_epi(with_stats):
                    def resid_epi(fb, bank):
                        S.op("dve", lambda: nc.vector.tensor_tensor(hT[:, fb, :], hT[:, fb, :], pb[bank][:], ALU.add), reads=[Bpb[bank], BhT[fb]], writes=[BhT[fb]])
                        if not with_stats:
                            return None

                        def deferred():
                            sq, Bsq = tf()
                            S.op("act", lambda: nc.scalar.activation(sq[:], hT[:, fb, :], AF.Square), reads=[BhT[fb]], writes=[Bsq])
                            S.op("pe", lambda: nc.tensor.matmul(pb[7][:], lhsT=ones_f, rhs=sq[:], start=(fb == 0), stop=(fb == NCH - 1)),
                                 reads=[Bsq, Bc], writes=[Bpb[7]])
                        return deferred
                    return resid_epi
                gemm_f(w_o[l], 0, 0, NCH, 1, lambda kc: (YT(kc), BR1[16 + kc]), make_resid_epi(True))

                if ti == 0:
                    dbg("ha%d" % l, hT[:], BhT, F32)
                norm_main(G_MLP[l], stats="bank7")
                for half in range(2):
                    def hid_epi(fb, bank):
                        r, Br = tf()
                        S.op("act", lambda: nc.scalar.activation(r[:], pb[bank][:], AF.Relu), reads=[Bpb[bank]], writes=[Br])
                        S.op("dve", lambda: nc.vector.tensor_tensor(R1[:, fb, :], r[:], r[:], ALU.mult), reads=[Br], writes=[BR1[fb]])
                        return None
                    gemm_f(w_mi[l], 0, half * 4096, 32, 1, act_xn, hid_epi)
                    gemm_f(w_mo[l], half * 4096, 0, NCH, 2, lambda kc: (R1[:, kc, :], BR1[kc]),
                           make_resid_epi(half == 1 and l == 0 and n_layers > 1))

                if ti == 0:
                    dbg("hm%d" % l, hT[:], BhT, F32)
            for blk in range(4):
                s = blk
                for g in range(4):
                    bk = next_bank()

                    def tr(g=g, bk=bk, blk=blk):
                        ins = None
                        for j in range(4):
                            c = g * 4 + j
                            ins = nc.tensor.transpose(pb[bk][:, j * 128:(j + 1) * 128], hT[:, c, blk * 128:(blk + 1) * 128], ident)
                        return ins
                    S.op("pe", tr, reads=BhT[g * 4:g * 4 + 4] + [Bc], writes=[Bpb[bk]])
                    S.op("act", lambda: nc.scalar.copy(xstage[s][:, 2 * g:2 * g + 2, :], pb[bk][:].rearrange("p (a b) -> p a b", b=256)),
                         reads=[Bpb[bk]], writes=Bxst[s][2 * g:2 * g + 2])
                S.dma("sp", out_d[t0 + blk * 128:t0 + (blk + 1) * 128, :].rearrange("p (a b) -> p a b", b=256), xstage[s],
                      reads=Bxst[s], writes=[Bout], dsem=os_d[s])
        for B, ds in dbg_list:
            S.e["sp"].wait_ge(S.dsem[ds], S.dcnt[ds])
        for s in range(4):
            if S.dcnt[os_d[s]]:
                S.e["sp"].wait_ge(S.dsem[os_d[s]], S.dcnt[os_d[s]])
    return nc


def _rel_bias_T(rel_bias):
    tbl = np.asarray(rel_bias, np.float32)
    kl = np.arange(640)[:, None]
    ql = np.arange(128)[None, :]
    dist = 512 + ql - kl
    idx = np.clip(dist, -256, 256) + 256
    ck, cq = kl // 64, ql // 64
    valid = (ck >= cq) & (ck <= cq + 8)
    g = tbl[idx]
    g = np.where(valid[:, :, None], g, np.float32(NEG))
    g = g.reshape(5, 128, 128, NH)
    g = g[::-1]
    out = np.transpose(g, (1, 3, 0, 2))
    return np.ascontiguousarray(out.reshape(128, NH * 5 * 128), dtype=np.float32)


def _consts():
    c = np.zeros((128, NCST), np.float32)
    c[:, C_ID:C_ID + 128] = np.eye(128, dtype=np.float32)
    c[:, C_ONE:C_ONE + 128] = 1.0
    j = np.arange(128)[:, None]
    s = np.arange(128)[None, :]
    c[:, C_TRI:C_TRI + 128] = (j >= s).astype(np.float32)
    c[:, C_CMP:C_CMP + 128] = (j < s).astype(np.float32)
    q = np.arange(512)[None, :]
    for r in range(4):
        c[:, C_MASK + r * 512:C_MASK + (r + 1) * 512] = ((r * 128 + j) < q).astype(np.float32)
    return c


def _gains(norm_attn, norm_mem, norm_mlp, norm_kv, qk_gain_a, qk_gain_mem):
    g = np.zeros((128, NG), np.float32)
    col = lambda v: np.asarray(v, np.float32).reshape(NCH, 128).T
    for l in range(2):
        g[:, G_ATTN[l]:G_ATTN[l] + 16] = col(norm_attn[l])
        g[:, G_MEM[l]:G_MEM[l] + 16] = col(norm_mem[l])
        g[:, G_MLP[l]:G_MLP[l] + 16] = col(norm_mlp[l])
    g[:, G_KV:G_KV + 16] = col(norm_kv)
    g[:, G_QA] = np.asarray(qk_gain_a, np.float32)[0, 0]
    g[:, G_QA + 1] = np.asarray(qk_gain_a, np.float32)[0, 1]
    for l in range(2):
        for j in range(2):
            g[:, G_QM + 2 * l + j] = np.asarray(qk_gain_mem, np.float32)[l, j]
    return g


_NC_CACHE = {}


def kernel(x, mem, norm_attn, norm_mem, norm_mlp, w_in_a, qk_gain_a, rel_bias, norm_kv, w_kv_shared, w_q_b,
           w_mem_kv, qk_gain_mem, w_o, w_mlp_in, w_mlp_out):
    f = lambda a: np.ascontiguousarray(np.asarray(a, dtype=np.float32))
    x, mem = f(x), f(mem)
    shared = {
        "w_in_a": f(w_in_a)[0], "w_kv_shared": f(w_kv_shared), "w_q_b": f(w_q_b)[0], "w_mem_kv": f(w_mem_kv),
        "w_o": f(w_o), "w_mlp_in": f(w_mlp_in), "w_mlp_out": f(w_mlp_out),
        "gains": _gains(norm_attn, norm_mem, norm_mlp, norm_kv, qk_gain_a, qk_gain_mem),
        "biasT": _rel_bias_T(np.asarray(rel_bias)[0]),
        "cst": _consts(),
    }
    if "nc" not in _NC_CACHE:
        _NC_CACHE["nc"] = build_program()
    nc = _NC_CACHE["nc"]
    in_maps = [dict(shared, x=x[b], mem=mem[b]) for b in range(8)]
    res = run_bass_kernel_spmd(nc, in_maps, core_ids=list(range(8)))
    return np.stack([np.asarray(res.results[b]["out"], dtype=np.float32).reshape(SEQ, D) for b in range(8)], axis=0)
```

```python
import numpy as np
from contextlib import ExitStack
import concourse.bass as bass
import concourse.mybir as mybir
from concourse.bass_utils import run_bass_kernel_spmd

F32 = mybir.dt.float32
BF16 = mybir.dt.bfloat16
AF = mybir.ActivationFunctionType
ALU = mybir.AluOpType

D = 2048
SEQ = 2048
T = 512
NT = SEQ // T
NCH = D // 128
HD = 128
NH = 12
NHM = 4
NMEM = 256
DFF = 8192
EPS = 1e-6
SCALE = HD ** -0.5
NEG = -30000.0

G_ATTN = (0, 16)
G_MEM = (32, 48)
G_MLP = (64, 80)
G_KV = 96
G_QA = 112
G_QM = 114
NG = 118
C_ID, C_ONE, C_TRI, C_CMP, C_MASK = 0, 128, 256, 384, 512
NCST = 512 + 4 * 512


class Buf:
    def __init__(self, name):
        self.name = name
        self.w = None
        self.r = {}


class Sched:
    ENG = ("pe", "dve", "act", "pool", "sp")

    def __init__(self, nc, es, n_dma_sems):
        self.nc = nc
        self.e = {"pe": nc.tensor, "dve": nc.vector, "act": nc.scalar, "pool": nc.gpsimd, "sp": nc.sync}
        self.sem = {k: es.enter_context(nc.semaphore("s_" + k)) for k in self.ENG}
        self.cnt = {k: 0 for k in self.ENG}
        self.dsem = [es.enter_context(nc.semaphore("d%d" % i)) for i in range(n_dma_sems)]
        self.dcnt = [0] * n_dma_sems
        self.ndsem = 0
        self.waited = {k: {} for k in self.ENG}

    def alloc_dsem(self):
        self.ndsem += 1
        return self.ndsem - 1

    def _wait(self, eng, tok):
        if tok is None:
            return
        if tok[0] == "c":
            if eng == "pe" and tok[1] == "pe":
                return
            key, sem, val = ("c", tok[1]), self.sem[tok[1]], tok[2]
        else:
            key, sem, val = ("d", tok[1]), self.dsem[tok[1]], tok[2]
        if self.waited[eng].get(key, 0) >= val:
            return
        self.waited[eng][key] = val
        self.e[eng].wait_ge(sem, val)

    def _deps(self, eng, reads, writes):
        for b in reads:
            self._wait(eng, b.w)
        for b in writes:
            self._wait(eng, b.w)
            for t in list(b.r.values()):
                self._wait(eng, t)

    def _commit(self, tok, reads, writes):
        for b in reads:
            b.r[(tok[0], tok[1])] = tok
        for b in writes:
            b.w = tok
            b.r = {}

    def op(self, eng, fn, reads=(), writes=()):
        self._deps(eng, reads, writes)
        ins = fn()
        self.cnt[eng] += 1
        ins.then_inc(self.sem[eng], 1)
        self._commit(("c", eng, self.cnt[eng]), reads, writes)

    def dma(self, q, out, in_, reads=(), writes=(), dsem=None):
        self._deps(q, reads, writes)
        self.dcnt[dsem] += 16
        self.e[q].dma_start(out=out, in_=in_).then_inc(self.dsem[dsem], 16)
        self._commit(("d", dsem, self.dcnt[dsem]), reads, writes)

    def final_wait(self, eng, bufs):
        for b in bufs:
            self._wait(eng, b.w)


def pipeline(items, stages, skew=1, reverse=False):
    n, ns = len(items), len(stages)
    for step in range(n + (ns - 1) * skew):
        for s in (range(ns - 1, -1, -1) if reverse else range(ns)):
            k = step - s * skew
            if 0 <= k < n:
                stages[s](items[k])


def build_program(n_tiles=NT, n_layers=2, debug=False):
    nc = bass.Bass("TRN2", target_bir_lowering=False)
    dram_in = lambda n, sh: nc.dram_tensor(n, sh, F32, kind="ExternalInput").ap()
    x_d = dram_in("x", [SEQ, D])
    mem_d = dram_in("mem", [NMEM, D])
    w_in_a = dram_in("w_in_a", [D, 5120])
    w_kv = dram_in("w_kv_shared", [D, 3072])
    w_q_b = dram_in("w_q_b", [D, 2048])
    w_mem = dram_in("w_mem_kv", [2, D, 1024])
    w_o = dram_in("w_o", [2, D, D])
    w_mi = dram_in("w_mlp_in", [2, D, DFF])
    w_mo = dram_in("w_mlp_out", [2, DFF, D])
    gains_d = dram_in("gains", [128, NG])
    bias_d = dram_in("biasT", [128, NH * 5 * 128])
    cst_d = dram_in("cst", [128, NCST])
    out_d = nc.dram_tensor("out", [SEQ, D], F32, kind="ExternalOutput").ap()
    Kc = [nc.dram_tensor("kc%d" % l, [NH, 128, SEQ], BF16).ap() for l in range(2)]
    Vc = [nc.dram_tensor("vc%d" % l, [NH, 128, SEQ // 128, 128], BF16).ap() for l in range(2)]

    with ExitStack() as es:
        S = Sched(nc, es, n_dma_sems=64)
        sb = lambda n, sh, dt: es.enter_context(nc.sbuf_tensor(n, sh, dt))
        ps = lambda n, sh, dt: es.enter_context(nc.psum_tensor(n, sh, dt))

        hT = sb("hT", [128, NCH, T], F32)
        BhT = [Buf("hT%d" % c) for c in range(NCH)]
        xnT = sb("xnT", [128, NCH, T], BF16)
        BxnT = [Buf("xn%d" % c) for c in range(NCH)]
        R1 = sb("R1", [128, 32, T], BF16)
        BR1 = [Buf("R1_%d" % i) for i in range(32)]
        R1f = R1[:].bitcast(F32)
        kst = [sb("kst%d" % i, [128, SEQ], BF16) for i in range(2)]
        vst = [sb("vst%d" % i, [128, SEQ // 128, 128], BF16) for i in range(2)]
        Bkst = [Buf("kst0"), Buf("kst1")]
        Bvst = [Buf("vst0"), Buf("vst1")]
        kvst_d = [S.alloc_dsem() for _ in range(4)]
        ktile = [sb("ktile%d" % i, [128, T], BF16) for i in range(2)]
        Bktile = [Buf("kt0"), Buf("kt1")]
        NVT = 4
        vtile = [sb("vtile%d" % i, [128, 256], BF16) for i in range(NVT)]
        Bvtile = [Buf("vt%d" % i) for i in range(NVT)]
        biasT = sb("biasT_sb", [128, NH * 5 * 128], BF16)
        Bbias = Buf("bias")
        kmT = [sb("kmT%d" % l, [128, NHM, NMEM], BF16) for l in range(2)]
        vm = [sb("vm%d" % l, [128, 2, 512], BF16) for l in range(2)]
        Bkm = [Buf("km0"), Buf("km1")]
        Bvm = [Buf("vm0"), Buf("vm1")]
        NF = 8
        f32p = [sb("f32p%d" % i, [128, T], F32) for i in range(NF)]
        Bf32p = [Buf("f32p%d" % i) for i in range(NF)]
        NB = 10
        b16p = [sb("b16p%d" % i, [128, T], BF16) for i in range(NB)]
        Bb16p = [Buf("b16p%d" % i) for i in range(NB)]
        rstd = sb("rstd", [128, T], F32)
        Brstd = Buf("rstd")
        gains = sb("gains_sb", [128, NG], F32)
        Bg = Buf("gains")
        cst = sb("cst_sb", [128, C_MASK], F32)
        Bc = Buf("cst")
        cstb = sb("cstb", [128, NCST], BF16)
        Bcb = Buf("cstb")
        rem = int(nc.sbuf_bytes_remaining)
        NSLAB = max(3, min(5, (rem - 4096) // 8192))
        slab = [sb("slab%d" % i, [128, 16, 256], BF16) for i in range(NSLAB)]
        Bslab = [Buf("slab%d" % i) for i in range(NSLAB)]
        slab_d = [S.alloc_dsem() for _ in range(NSLAB)]
        pb = [ps("pb%d" % i, [128, T], F32) for i in range(8)]
        Bpb = [Buf("pb%d" % i) for i in range(8)]

        ident = cst[:, C_ID:C_ID + 128]
        ones_f = cst[:, C_ONE:C_ONE + 128]
        ones_b = cstb[:, C_ONE:C_ONE + 128]
        tri_b = cstb[:, C_TRI:C_TRI + 128]
        cmp_b = cstb[:, C_CMP:C_CMP + 128]
        maskb = lambda r: cstb[:, C_MASK + r * 512:C_MASK + (r + 1) * 512]

        st = {"f": 0, "b": 0, "slab": 0}
        dbg_list = []

        def dbg(name, ap, bufs, dt):
            if not debug:
                return
            shp = list(ap.shape)
            dtn = nc.dram_tensor("dbg_" + name, shp, dt, kind="ExternalOutput").ap()
            ds = S.alloc_dsem()
            B = Buf("dbg_" + name)
            S.dma("sp", dtn, ap, reads=bufs, writes=[B], dsem=ds)
            dbg_list.append((B, ds))

        def tf():
            i = st["f"] % NF
            st["f"] += 1
            return f32p[i], Bf32p[i]

        def tb():
            i = st["b"] % NB
            st["b"] += 1
            return b16p[i], Bb16p[i]

        S.dma("sp", gains[:], gains_d[:, :], writes=[Bg], dsem=S.alloc_dsem())
        S.dma("sp", cst[:], cst_d[:, 0:C_MASK], writes=[Bc], dsem=S.alloc_dsem())
        S.dma("pool", cstb[:], cst_d[:, :], writes=[Bcb], dsem=S.alloc_dsem())
        S.dma("pool", biasT[:], bias_d[:, :], writes=[Bbias], dsem=S.alloc_dsem())
        S.op("dve", lambda: nc.vector.tensor_scalar(gains[:, 0:G_QA], gains[:, 0:G_QA], float(np.sqrt(D)), None, ALU.mult),
             reads=[Bg], writes=[Bg])
        S.op("dve", lambda: nc.vector.tensor_scalar(gains[:, G_QA:NG], gains[:, G_QA:NG], float(np.sqrt(HD)), None, ALU.mult),
             reads=[Bg], writes=[Bg])

        def load_slab(src_ap):
            i = st["slab"] % NSLAB
            st["slab"] += 1
            S.dma("pool", slab[i][:], src_ap, writes=[Bslab[i]], dsem=slab_d[i])
            return slab[i], Bslab[i]

        def wview(w2d, r0, c0):
            return w2d[r0:r0 + 2048, c0:c0 + 256].rearrange("(kc p) f -> p kc f", p=128)

        gemm_banks = [0, 1, 2, 3, 4, 5]
        gstate = {"bank": 0, "stat": 0}

        def next_bank():
            i = gemm_banks[gstate["bank"] % len(gemm_banks)]
            gstate["bank"] += 1
            return i

        def next_stat():
            i = 6 + gstate["stat"] % 2
            gstate["stat"] += 1
            return i

        def gemm_f(w2d, r0, c0, nfb, kslabs, act, epi, N=T):
            pending = []
            for j in range(nfb // 2):
                banks = [next_bank(), next_bank()]
                for ks in range(kslabs):
                    sl, Bsl = load_slab(wview(w2d, r0 + ks * 2048, c0 + j * 256))
                    for f in range(2):
                        def mm(f=f, ks=ks, sl=sl):
                            ins = None
                            for kc in range(16):
                                a, _ = act(ks * 16 + kc)
                                ins = nc.tensor.matmul(pb[banks[f]][:, 0:N], lhsT=sl[:, kc, f * 128:(f + 1) * 128], rhs=a,
                                                       start=(ks == 0 and kc == 0), stop=(ks == kslabs - 1 and kc == 15))
                            return ins
                        if j == 0 and ks == 0 and f == 0:
                            for kc in range(16):
                                a, Ba = act(kc)
                                S.op("pe", lambda: nc.tensor.matmul(pb[banks[f]][:, 0:N], lhsT=sl[:, kc, f * 128:(f + 1) * 128], rhs=a,
                                                                    start=(kc == 0), stop=(kslabs == 1 and kc == 15)),
                                     reads=[Bsl, Ba], writes=[Bpb[banks[f]]])
                            continue
                        rb = [Bsl] + [act(ks * 16 + kc)[1] for kc in range(16)]
                        S.op("pe", mm, reads=rb, writes=[Bpb[banks[f]]])
                newp = []
                for f in range(2):
                    d = epi(2 * j + f, banks[f])
                    if d is not None:
                        newp.append(d)
                for d in pending:
                    d()
                pending = newp
            for d in pending:
                d()

        def gemm_t(w2d, c0, nslabs, act, epi):
            for j in range(nslabs):
                sl, Bsl = load_slab(wview(w2d, 0, c0 + j * 256))
                for tblk in range(T // 128):
                    bk = next_bank()

                    def mm(sl=sl, tblk=tblk, bk=bk):
                        ins = None
                        for kc in range(16):
                            ins = nc.tensor.matmul(pb[bk][:, 0:256], lhsT=act(kc)[0][:, tblk * 128:(tblk + 1) * 128], rhs=sl[:, kc, :],
                                                   start=(kc == 0), stop=(kc == 15))
                        return ins
                    S.op("pe", mm, reads=[Bsl] + [act(kc)[1] for kc in range(16)], writes=[Bpb[bk]])
                    epi(j, tblk, bk)

        def act_xn(kc):
            return xnT[:, kc, :], BxnT[kc]

        def norm_main(gcol, N=T, src=None, Bsrc=None, dst=None, Bdst=None, stats="compute"):
            src = hT if src is None else src
            Bsrc = BhT if Bsrc is None else Bsrc
            dst = xnT if dst is None else dst
            Bdst = BxnT if Bdst is None else Bdst
            if stats == "compute":
                sbk = next_stat()
                for c in range(NCH):
                    sq, Bsq = tf()
                    S.op("act", lambda: nc.scalar.activation(sq[:, 0:N], src[:, c, 0:N], AF.Square), reads=[Bsrc[c]], writes=[Bsq])
                    S.op("pe", lambda: nc.tensor.matmul(pb[sbk][:, 0:N], lhsT=ones_f, rhs=sq[:, 0:N], start=(c == 0), stop=(c == NCH - 1)),
                         reads=[Bsq, Bc], writes=[Bpb[sbk]])
            elif stats == "bank7":
                sbk = 7
            if stats != "reuse":
                lt, Blt = tf()
                S.op("act", lambda: nc.scalar.activation(lt[:, 0:N], pb[sbk][:, 0:N], AF.Ln, bias=float(D * EPS)), reads=[Bpb[sbk]], writes=[Blt])
                S.op("act", lambda: nc.scalar.activation(rstd[:, 0:N], lt[:, 0:N], AF.Exp, scale=-0.5), reads=[Blt], writes=[Brstd])
            for c in range(NCH):
                S.op("dve", lambda: nc.vector.scalar_tensor_tensor(dst[:, c, 0:N], src[:, c, 0:N], gains[:, gcol + c:gcol + c + 1], rstd[:, 0:N],
                                                                   ALU.mult, ALU.mult),
                     reads=[Bsrc[c], Brstd, Bg], writes=[Bdst[c]])

        def headnorm_epi(bank, gcol, dst_ap, Bdst, N=T, after=None):
            sq, Bsq = tf()
            S.op("act", lambda: nc.scalar.activation(sq[:, 0:N], pb[bank][:, 0:N], AF.Square), reads=[Bpb[bank]], writes=[Bsq])

            def deferred():
                sbk = next_stat()
                S.op("pe", lambda: nc.tensor.matmul(pb[sbk][:, 0:N], lhsT=ones_f, rhs=sq[:, 0:N], start=True, stop=True),
                     reads=[Bsq, Bc], writes=[Bpb[sbk]])
                r, Br = sq, Bsq
                S.op("act", lambda: nc.scalar.activation(r[:, 0:N], pb[sbk][:, 0:N], AF.Ln, bias=float(HD * EPS)), reads=[Bpb[sbk]], writes=[Br])
                S.op("act", lambda: nc.scalar.activation(r[:, 0:N], r[:, 0:N], AF.Exp, scale=-0.5), reads=[Br], writes=[Br])
                S.op("dve", lambda: nc.vector.scalar_tensor_tensor(dst_ap, pb[bank][:, 0:N], gains[:, gcol:gcol + 1], r[:, 0:N], ALU.mult, ALU.mult),
                     reads=[Bpb[bank], Br, Bg], writes=[Bdst])
                if after is not None:
                    after()
            return deferred

        def softmax_attn(items):
            flat = []
            for it in items:
                nb = len(it["blocks"])
                for bi, blk in enumerate(it["blocks"]):
                    flat.append((it, bi, blk, bi == 0, bi == nb - 1))
            sc_banks = [0, 1, 2]
            cnt = {"s": 0}
            hold = {}

            def stA(e):
                it, bi, (kl, v, c0, c1, bias), first, last = e
                bk = sc_banks[cnt["s"] % 3]
                cnt["s"] += 1
                if first and it.get("load") is not None:
                    it["load"]()
                S.op("pe", lambda: nc.tensor.matmul(pb[bk][:, c0:c1], lhsT=kl, rhs=it["q"](c0, c1), start=True, stop=True),
                     reads=it["reads"], writes=[Bpb[bk]])
                p, Bp = tb()
                if bias is not None:
                    t, Bt = tf()
                    S.op("dve", lambda: nc.vector.scalar_tensor_tensor(t[:, c0:c1], pb[bk][:, c0:c1], SCALE, bias, ALU.mult, ALU.add),
                         reads=[Bpb[bk], Bbias], writes=[Bt])
                    S.op("act", lambda: nc.scalar.activation(p[:, c0:c1], t[:, c0:c1], AF.Exp), reads=[Bt], writes=[Bp])
                else:
                    S.op("act", lambda: nc.scalar.activation(p[:, c0:c1], pb[bk][:, c0:c1], AF.Exp, scale=SCALE), reads=[Bpb[bk]], writes=[Bp])
                hold[id(e)] = (p, Bp)

            def stB(e):
                it, bi, (kl, v, c0, c1, bias), first, last = e
                p, Bp = hold.pop(id(e))
                ob, db = it["ob"], it["db"]

                def mm():
                    nc.tensor.matmul(pb[db][:, c0:c1], lhsT=ones_b, rhs=p[:, c0:c1], start=first, stop=last, skip_group_check=True)
                    return nc.tensor.matmul(pb[ob][:, c0:c1], lhsT=v, rhs=p[:, c0:c1], start=first, stop=last, skip_group_check=True)
                S.op("pe", mm, reads=[Bp, Bcb] + it["reads"], writes=[Bpb[ob], Bpb[db]])
                if last:
                    r, Br = tf()
                    S.op("act", lambda: nc.scalar.activation(r[:], pb[db][:], AF.Ln), reads=[Bpb[db]], writes=[Br])
                    S.op("act", lambda: nc.scalar.activation(r[:], r[:], AF.Exp, scale=-1.0), reads=[Br], writes=[Br])
                    S.op("dve", lambda: nc.vector.tensor_tensor(it["dst"], pb[ob][:], r[:], ALU.mult), reads=[Bpb[ob], Br], writes=[it["Bdst"]])
            pipeline(flat, [stA, stB], skew=2)

        def load_kv(l, h, tok0, tok1, par):
            S.dma("sp", kst[par][:, tok0:tok1], Kc[l][h, :, tok0:tok1], reads=[b for t_ in range(tok0 // T, (tok1 - 1) // T + 1) for b in BKc[l][t_]],
                  writes=[Bkst[par]], dsem=kvst_d[par])
            S.dma("sp", vst[par][:, tok0 // 128:tok1 // 128, :], Vc[l][h, :, tok0 // 128:tok1 // 128, :],
                  reads=[b for t_ in range(tok0 // T, (tok1 - 1) // T + 1) for b in BVc[l][t_]], writes=[Bvst[par]], dsem=kvst_d[2 + par])

        BKc = [[[Buf("Kc%d_%d_%d" % (l, t_, p_)) for p_ in range(2)] for t_ in range(NT)] for l in range(2)]
        BVc = [[[Buf("Vc%d_%d_%d" % (l, t_, p_)) for p_ in range(NVT)] for t_ in range(NT)] for l in range(2)]
        kc_d = [S.alloc_dsem() for _ in range(2)]
        vc_d = [S.alloc_dsem() for _ in range(NVT)]
        kvst2_d = [S.alloc_dsem() for _ in range(4)]

        QT = lambda h: R1[:, h, :]
        QMT = lambda h: R1[:, 12 + h, :]
        YT = lambda j: R1[:, 16 + j, :]

        memT = lambda c: R1f[:, c, :]
        xs_d = [S.alloc_dsem(), S.alloc_dsem()]
        xstage = [R1f[:, 16:24, :], R1f[:, 24:32, :]]
        Bxst = [BR1[16:24], BR1[24:32]]
        for blk in range(2):
            S.dma("sp", xstage[blk], mem_d[blk * 128:(blk + 1) * 128, :].rearrange("p (a b) -> p a b", b=256), writes=Bxst[blk], dsem=xs_d[blk])
            xs2 = xstage[blk].rearrange("p a b -> p (a b)")
            for g in range(4):
                bk = next_bank()

                def tr(g=g, bk=bk):
                    ins = None
                    for j in range(4):
                        c = g * 4 + j
                        ins = nc.tensor.transpose(pb[bk][:, j * 128:(j + 1) * 128], xs2[:, c * 128:(c + 1) * 128], ident)
                    return ins
                S.op("pe", tr, reads=Bxst[blk] + [Bc], writes=[Bpb[bk]])
                S.op("act", lambda: nc.scalar.copy(R1f[:, g * 4:g * 4 + 4, blk * 128:(blk + 1) * 128],
                                                   pb[bk][:].rearrange("p (j t) -> p j t", t=128)),
                     reads=[Bpb[bk]], writes=BR1[g * 4:g * 4 + 4])
        mem_n = xnT
        Bmem_n = BxnT
        for l in range(n_layers):
            norm_main(G_MEM[l], N=NMEM, src=R1f, Bsrc=BR1[0:16], dst=mem_n, Bdst=Bmem_n, stats=("compute" if l == 0 else "reuse"))
            act_m = lambda kc: (mem_n[:, kc, 0:NMEM], Bmem_n[kc])

            def epi_km(fb, bank, l=l):
                return headnorm_epi(bank, G_QM + 2 * l + 1, kmT[l][:, fb, :], Bkm[l], N=NMEM)
            gemm_f(w_mem[l], 0, 0, NHM, 1, act_m, epi_km, N=NMEM)
            for j in range(2):
                sl, Bsl = load_slab(wview(w_mem[l], 0, 512 + j * 256))
                for blk in range(2):
                    bk = next_bank()

                    def mm(sl=sl, blk=blk, bk=bk):
                        ins = None
                        for kc in range(16):
                            ins = nc.tensor.matmul(pb[bk][:, 0:256], lhsT=mem_n[:, kc, blk * 128:(blk + 1) * 128], rhs=sl[:, kc, :],
                                                   start=(kc == 0), stop=(kc == 15))
                        return ins
                    S.op("pe", mm, reads=[Bsl] + Bmem_n, writes=[Bpb[bk]])
                    S.op("act", lambda: nc.scalar.copy(vm[l][:, blk, j * 256:(j + 1) * 256], pb[bk][:, 0:256]), reads=[Bpb[bk]], writes=[Bvm[l]])

        os_d = [S.alloc_dsem(), S.alloc_dsem()]
        Bout = Buf("out")
        for ti in range(n_tiles):
            t0 = ti * T
            for blk in range(4):
                s = blk % 2
                S.dma("sp", xstage[s], x_d[t0 + blk * 128:t0 + (blk + 1) * 128, :].rearrange("p (a b) -> p a b", b=256), writes=Bxst[s], dsem=xs_d[s])
                xs2 = xstage[s].rearrange("p a b -> p (a b)")
                for g in range(4):
                    bk = next_bank()

                    def tr(g=g, bk=bk, xs2=xs2):
                        ins = None
                        for j in range(4):
                            c = g * 4 + j
                            ins = nc.tensor.transpose(pb[bk][:, j * 128:(j + 1) * 128], xs2[:, c * 128:(c + 1) * 128], ident)
                        return ins
                    S.op("pe", tr, reads=Bxst[s] + [Bc], writes=[Bpb[bk]])
                    S.op("act", lambda: nc.scalar.copy(hT[:, g * 4:g * 4 + 4, blk * 128:(blk + 1) * 128],
                                                       pb[bk][:].rearrange("p (j t) -> p j t", t=128)),
                         reads=[Bpb[bk]], writes=BhT[g * 4:g * 4 + 4])

            if ti == 0:
                dbg("h0", hT[:], BhT, F32)
            for l in range(n_layers):
                kv_w, kv_c0 = (w_in_a, 1536) if l == 0 else (w_kv, 0)
                if l == 1:
                    norm_main(G_KV, stats="bank7")
                else:
                    norm_main(G_ATTN[0])

                def k_epi(fb, bank, l=l):
                    kt, Bkt = ktile[fb % 2], Bktile[fb % 2]

                    def store(fb=fb, kt=kt, Bkt=Bkt):
                        S.dma("sp", Kc[l][fb, :, t0:t0 + T], kt[:], reads=[Bkt], writes=[BKc[l][ti][fb % 2]], dsem=kc_d[fb % 2])
                    if l == 0:
                        return headnorm_epi(bank, G_QA + 1, kt[:], Bkt, after=store)
                    S.op("act", lambda: nc.scalar.copy(kt[:], pb[bank][:]), reads=[Bpb[bank]], writes=[Bkt])
                    store()
                    return None

                def v_epi(j, tblk, bk, l=l):
                    vp = (j * 4 + tblk) % NVT
                    vt, Bvt = vtile[vp], Bvtile[vp]
                    S.op("act", lambda: nc.scalar.copy(vt[:], pb[bk][:, 0:256]), reads=[Bpb[bk]], writes=[Bvt])
                    S.dma("sp", Vc[l][2 * j:2 * j + 2, :, ti * 4 + tblk, :].rearrange("h p d -> p h d"), vt[:].rearrange("p (h d) -> p h d", d=128),
                          reads=[Bvt], writes=[BVc[l][ti][vp]], dsem=vc_d[vp])

                if l == 0:
                    def q_epi(fb, bank):
                        if fb < NH:
                            return headnorm_epi(bank, G_QA, QT(fb), BR1[fb])
                        return k_epi(fb - NH, bank)
                    gemm_f(w_in_a, 0, 0, 2 * NH, 1, act_xn, q_epi)
                    gemm_t(w_in_a, 3072, 6, act_xn, v_epi)

                    def qm_epi(fb, bank):
                        return headnorm_epi(bank, G_QM + 0, QMT(fb), BR1[12 + fb])
                    gemm_f(w_in_a, 0, 4608, NHM, 1, act_xn, qm_epi)
                else:
                    gemm_f(w_kv, 0, 0, NH, 1, act_xn, k_epi)
                    gemm_t(w_kv, 1536, 6, act_xn, v_epi)
                    norm_main(G_ATTN[1], stats="reuse")

                    def q1_epi(fb, bank):
                        if fb < NH:
                            S.op("act", lambda: nc.scalar.copy(QT(fb), pb[bank][:]), reads=[Bpb[bank]], writes=[BR1[fb]])
                            return None
                        return headnorm_epi(bank, G_QM + 2, QMT(fb - NH), BR1[12 + fb - NH])
                    gemm_f(w_q_b, 0, 0, NH + NHM, 1, act_xn, q1_epi)

                if ti == 0:
                    dbg("xn%d" % l, xnT[:], BxnT, BF16)
                    dbg("q%d" % l, R1[:, 0:16, :], BR1[0:16], BF16)
                if l == 0:
                    items = []
                    for h in range(NH):
                        par = h % 2
                        lo = max(0, t0 - T)
                        blocks = []
                        for kbi in range(8):
                            kb = 4 * (ti - 1) + kbi
                            if kb < 0:
                                continue
                            g_lo, g_hi = max(0, kbi - 4), min(3, kbi)
                            c0, c1 = g_lo * 128, (g_hi + 1) * 128
                            rlo = 4 - kbi + g_lo
                            bofs = (h * 5 + rlo) * 128
                            blocks.append((kst[par][:, kb * 128:(kb + 1) * 128], vst[par][:, kb, :], c0, c1, biasT[:, bofs:bofs + (c1 - c0)]))
                        items.append(dict(blocks=blocks, load=(lambda h=h, lo=lo, par=par: load_kv(0, h, lo, t0 + T, par)),
                                          q=(lambda c0, c1, h=h: R1[:, h, c0:c1]), reads=[BR1[h], Bkst[par], Bvst[par]],
                                          dst=YT(h), Bdst=BR1[16 + h], ob=3 + 2 * par, db=4 + 2 * par))
                    softmax_attn(items)
                else:
                    def stage_set(si):
                        if si < 2:
                            return kst[si], vst[si], [Bkst[si]], [Bvst[si]], kvst_d[si], kvst_d[2 + si]
                        b0 = (si - 2) * 8
                        kv_ = xnT[:, b0:b0 + 4, :].rearrange("p a b -> p (a b)")
                        vv_ = xnT[:, b0 + 4:b0 + 8, :].rearrange("p a (b c) -> p (a b) c", c=128)
                        return kv_, vv_, BxnT[b0:b0 + 4], BxnT[b0 + 4:b0 + 8], kvst2_d[(si - 2) * 2], kvst2_d[(si - 2) * 2 + 1]
                    flat = []
                    nkb = 4 * (ti + 1)
                    for hp in range(NH // 2):
                        for kb in range(nkb - 1, -1, -1):
                            for h in (2 * hp, 2 * hp + 1):
                                r = kb - 4 * ti
                                flat.append(dict(h=h, kb=kb, first=(kb == nkb - 1), last=(kb == 0), diag=(r >= 0), r=r,
                                                 c0=(r * 128 if r >= 0 else 0), ss=stage_set(h % 4)))
                    cz = {"n": 0, "e": 0, "x": 0, "sp": 0, "a": 0}

                    def s0(e):
                        h, kb, c0 = e["h"], e["kb"], e["c0"]
                        kv_, vv_, Bk_, Bv_, dk, dv = e["ss"]
                        if e["first"]:
                            ntok = t0 + T
                            S.dma("sp", kv_[:, 0:ntok], Kc[1][h, :, 0:ntok], reads=[b for t_ in range(ti + 1) for b in BKc[1][t_]], writes=Bk_, dsem=dk)
                            S.dma("sp", vv_[:, 0:ntok // 128, :], Vc[1][h, :, 0:ntok // 128, :], reads=[b for t_ in range(ti + 1) for b in BVc[1][t_]],
                                  writes=Bv_, dsem=dv)
                        bk = cz["n"] % 3
                        cz["n"] += 1
                        S.op("pe", lambda: nc.tensor.matmul(pb[bk][:, c0:T], lhsT=kv_[:, kb * 128:(kb + 1) * 128], rhs=R1[:, h, c0:T], start=True, stop=True),
                             reads=Bk_ + [BR1[h]], writes=[Bpb[bk]])
                        ei = cz["e"] % 6
                        cz["e"] += 1
                        ex, Bex = f32p[ei], Bf32p[ei]
                        S.op("act", lambda: nc.scalar.activation(ex[:, c0:T], pb[bk][:, c0:T], AF.Exp, scale=SCALE), reads=[Bpb[bk]], writes=[Bex])
                        if e["diag"]:
                            S.op("dve", lambda: nc.vector.tensor_tensor(ex[:, c0:T], ex[:, c0:T], maskb(e["r"])[:, c0:T], ALU.mult), reads=[Bex, Bcb], writes=[Bex])
                        e["ex"] = (ex, Bex)

                    def s1(e):
                        c0 = e["c0"]
                        ex, Bex = e["ex"]
                        si = cz["sp"] % 6
                        cz["sp"] += 1
                        spb, Bspb = b16p[si], Bb16p[si]
                        S.op("act", lambda: nc.scalar.activation(spb[:, c0:T], ex[:, c0:T], AF.Ln, bias=1.0), reads=[Bex], writes=[Bspb])
                        e["sp"] = (spb, Bspb)

                    def s2(e):
                        c0 = e["c0"]
                        ex, Bex = e["ex"]
                        spb, Bspb = e["sp"]
                        ab = 3 + e["h"] % 2
                        S.op("pe", lambda: nc.tensor.matmul(pb[ab][:, c0:T], lhsT=tri_b, rhs=spb[:, c0:T], start=e["first"], stop=False, skip_group_check=True),
                             reads=[Bspb, Bcb], writes=[Bpb[ab]])
                        xi = 6 + cz["x"] % 2
                        cz["x"] += 1
                        x2, Bx2 = f32p[xi], Bf32p[xi]
                        S.op("act", lambda: nc.scalar.activation(x2[:, c0:T], pb[ab][:, c0:T], AF.Exp, scale=-1.0), reads=[Bpb[ab]], writes=[Bx2])
                        ai = 6 + cz["a"] % 4
                        cz["a"] += 1
                        aT, BaT = b16p[ai], Bb16p[ai]
                        S.op("dve", lambda: nc.vector.tensor_tensor(aT[:, c0:T], ex[:, c0:T], x2[:, c0:T], ALU.mult), reads=[Bex, Bx2], writes=[BaT])
                        e["a"] = (aT, BaT)

                    def s3(e):
                        h, kb, c0 = e["h"], e["kb"], e["c0"]
                        kv_, vv_, Bk_, Bv_, dk, dv = e["ss"]
                        spb, Bspb = e["sp"]
                        aT, BaT = e["a"]
                        ab, ob = 3 + h % 2, 5 + h % 2

                        def mm():
                            if not e["last"]:
                                nc.tensor.matmul(pb[ab][:, c0:T], lhsT=cmp_b, rhs=spb[:, c0:T], start=False, stop=False, skip_group_check=True)
                            return nc.tensor.matmul(pb[ob][:, c0:T], lhsT=vv_[:, kb, :], rhs=aT[:, c0:T], start=e["first"], stop=e["last"], skip_group_check=True)
                        S.op("pe", mm, reads=[Bspb, BaT, Bcb] + Bv_, writes=[Bpb[ab], Bpb[ob]])
                        if e["last"]:
                            S.op("act", lambda: nc.scalar.copy(YT(h), pb[ob][:]), reads=[Bpb[ob]], writes=[BR1[16 + h]])
                    pipeline(flat, [s0, s1, s2, s3], skew=2, reverse=True)

                items = []
                for hm in range(NHM):
                    blocks = [(kmT[l][:, hm, b * 128:(b + 1) * 128], vm[l][:, b, hm * 128:(hm + 1) * 128], 0, T, None) for b in range(2)]
                    items.append(dict(blocks=blocks, q=(lambda c0, c1, hm=hm: R1[:, 12 + hm, c0:c1]), reads=[BR1[12 + hm], Bkm[l], Bvm[l]],
                                      dst=YT(NH + hm), Bdst=BR1[16 + NH + hm], ob=3 + 2 * (hm % 2), db=4 + 2 * (hm % 2)))
                softmax_attn(items)

                if ti == 0:
                    dbg("y%d" % l, R1[:, 16:32, :], BR1[16:32], BF16)
                def make_resid_epi(with_stats):
                    def resid_epi(fb, bank):
                        S.op("dve", lambda: nc.vector.tensor_tensor(hT[:, fb, :], hT[:, fb, :], pb[bank][:], ALU.add), reads=[Bpb[bank], BhT[fb]], writes=[BhT[fb]])
                        if not with_stats:
                            return None

                        def deferred():
                            sq, Bsq = tf()
                            S.op("act", lambda: nc.scalar.activation(sq[:], hT[:, fb, :], AF.Square), reads=[BhT[fb]], writes=[Bsq])
                            S.op("pe", lambda: nc.tensor.matmul(pb[7][:], lhsT=ones_f, rhs=sq[:], start=(fb == 0), stop=(fb == NCH - 1)),
                                 reads=[Bsq, Bc], writes=[Bpb[7]])
                        return deferred
                    return resid_epi
                gemm_f(w_o[l], 0, 0, NCH, 1, lambda kc: (YT(kc), BR1[16 + kc]), make_resid_epi(True))

                if ti == 0:
                    dbg("ha%d" % l, hT[:], BhT, F32)
                norm_main(G_MLP[l], stats="bank7")
                for half in range(2):
                    def hid_epi(fb, bank):
                        r, Br = tf()
                        S.op("act", lambda: nc.scalar.activation(r[:], pb[bank][:], AF.Relu), reads=[Bpb[bank]], writes=[Br])
                        S.op("dve", lambda: nc.vector.tensor_tensor(R1[:, fb, :], r[:], r[:], ALU.mult), reads=[Br], writes=[BR1[fb]])
                        return None
                    gemm_f(w_mi[l], 0, half * 4096, 32, 1, act_xn, hid_epi)
                    gemm_f(w_mo[l], half * 4096, 0, NCH, 2, lambda kc: (R1[:, kc, :], BR1[kc]),
                           make_resid_epi(half == 1 and l == 0 and n_layers > 1))

                if ti == 0:
                    dbg("hm%d" % l, hT[:], BhT, F32)
            for blk in range(4):
                s = blk % 2
                for g in range(4):
                    bk = next_bank()

                    def tr(g=g, bk=bk, blk=blk):
                        ins = None
                        for j in range(4):
                            c = g * 4 + j
                            ins = nc.tensor.transpose(pb[bk][:, j * 128:(j + 1) * 128], hT[:, c, blk * 128:(blk + 1) * 128], ident)
                        return ins
                    S.op("pe", tr, reads=BhT[g * 4:g * 4 + 4] + [Bc], writes=[Bpb[bk]])
                    S.op("act", lambda: nc.scalar.copy(xstage[s][:, 2 * g:2 * g + 2, :], pb[bk][:].rearrange("p (a b) -> p a b", b=256)),
                         reads=[Bpb[bk]], writes=Bxst[s][2 * g:2 * g + 2])
                S.dma("sp", out_d[t0 + blk * 128:t0 + (blk + 1) * 128, :].rearrange("p (a b) -> p a b", b=256), xstage[s],
                      reads=Bxst[s], writes=[Bout], dsem=os_d[s])
        for B, ds in dbg_list:
            S.e["sp"].wait_ge(S.dsem[ds], S.dcnt[ds])
        for s in range(2):
            if S.dcnt[os_d[s]]:
                S.e["sp"].wait_ge(S.dsem[os_d[s]], S.dcnt[os_d[s]])
    return nc


def _rel_bias_T(rel_bias):
    tbl = np.asarray(rel_bias, np.float32)
    kl = np.arange(640)[:, None]
    ql = np.arange(128)[None, :]
    dist = 512 + ql - kl
    idx = np.clip(dist, -256, 256) + 256
    ck, cq = kl // 64, ql // 64
    valid = (ck >= cq) & (ck <= cq + 8)
    g = tbl[idx]
    g = np.where(valid[:, :, None], g, np.float32(NEG))
    g = g.reshape(5, 128, 128, NH)
    g = g[::-1]
    out = np.transpose(g, (1, 3, 0, 2))
    return np.ascontiguousarray(out.reshape(128, NH * 5 * 128), dtype=np.float32)


def _consts():
    c = np.zeros((128, NCST), np.float32)
    c[:, C_ID:C_ID + 128] = np.eye(128, dtype=np.float32)
    c[:, C_ONE:C_ONE + 128] = 1.0
    j = np.arange(128)[:, None]
    s = np.arange(128)[None, :]
    c[:, C_TRI:C_TRI + 128] = (j >= s).astype(np.float32)
    c[:, C_CMP:C_CMP + 128] = (j < s).astype(np.float32)
    q = np.arange(512)[None, :]
    for r in range(4):
        c[:, C_MASK + r * 512:C_MASK + (r + 1) * 512] = ((r * 128 + j) < q).astype(np.float32)
    return c


def _gains(norm_attn, norm_mem, norm_mlp, norm_kv, qk_gain_a, qk_gain_mem):
    g = np.zeros((128, NG), np.float32)
    col = lambda v: np.asarray(v, np.float32).reshape(NCH, 128).T
    for l in range(2):
        g[:, G_ATTN[l]:G_ATTN[l] + 16] = col(norm_attn[l])
        g[:, G_MEM[l]:G_MEM[l] + 16] = col(norm_mem[l])
        g[:, G_MLP[l]:G_MLP[l] + 16] = col(norm_mlp[l])
    g[:, G_KV:G_KV + 16] = col(norm_kv)
    g[:, G_QA] = np.asarray(qk_gain_a, np.float32)[0, 0]
    g[:, G_QA + 1] = np.asarray(qk_gain_a, np.float32)[0, 1]
    for l in range(2):
        for j in range(2):
            g[:, G_QM + 2 * l + j] = np.asarray(qk_gain_mem, np.float32)[l, j]
    return g


_NC_CACHE = {}


def kernel(x, mem, norm_attn, norm_mem, norm_mlp, w_in_a, qk_gain_a, rel_bias, norm_kv, w_kv_shared, w_q_b,
           w_mem_kv, qk_gain_mem, w_o, w_mlp_in, w_mlp_out):
    f = lambda a: np.ascontiguousarray(np.asarray(a, dtype=np.float32))
    x, mem = f(x), f(mem)
    shared = {
        "w_in_a": f(w_in_a)[0], "w_kv_shared": f(w_kv_shared), "w_q_b": f(w_q_b)[0], "w_mem_kv": f(w_mem_kv),
        "w_o": f(w_o), "w_mlp_in": f(w_mlp_in), "w_mlp_out": f(w_mlp_out),
        "gains": _gains(norm_attn, norm_mem, norm_mlp, norm_kv, qk_gain_a, qk_gain_mem),
        "biasT": _rel_bias_T(np.asarray(rel_bias)[0]),
        "cst": _consts(),
    }
    if "nc" not in _NC_CACHE:
        _NC_CACHE["nc"] = build_program()
    nc = _NC_CACHE["nc"]
    in_maps = [dict(shared, x=x[b], mem=mem[b]) for b in range(8)]
    res = run_bass_kernel_spmd(nc, in_maps, core_ids=list(range(8)))
    return np.stack([np.asarray(res.results[b]["out"], dtype=np.float32).reshape(SEQ, D) for b in range(8)], axis=0)
```
